# Optimizing a Trainium2 kernel written in Bass

```python
import jax, jax.numpy as jnp
from jax import lax
import numpy as np

D_MODEL = 2048
BATCH = 4
SEQ = 2048
DEPTH = 4
DEC_BATCH = 128
DEC_SEQ = 1
PAST_LEN = 16384
PAGE_SIZE = 128

HEAD_DIM = 128
POOL_WIDTH = D_MODEL // 4
POOL_GROUPS = 4
POOL_GROUP_DIM = POOL_WIDTH // POOL_GROUPS
POOL_WINDOWS = (2, 4, 8, 16)
POOL_STATE = max(POOL_WINDOWS) - 1
SCONV_WIDTH = 3 * D_MODEL // 8
SCONV_K = 3
CCONV_WIDTH = D_MODEL - POOL_WIDTH - SCONV_WIDTH
CCONV_K = 31
D_IN = POOL_WIDTH + 3 * SCONV_WIDTH + 2 * CCONV_WIDTH
D_FF = -(-8 * D_MODEL // (3 * 256)) * 256
EPS = 1e-6

kernel_name = "hybrid_pool_shortconv_conformer_decode_step"


def _rmsnorm(x, g):
    xf = x.astype(jnp.float32)
    y = xf * lax.rsqrt(jnp.mean(xf * xf, axis=-1, keepdims=True) + EPS)
    return (y * g.astype(jnp.float32)).astype(x.dtype)


def _layernorm(x, g, b):
    xf = x.astype(jnp.float32)
    mu = jnp.mean(xf, axis=-1, keepdims=True)
    var = jnp.mean(jnp.square(xf - mu), axis=-1, keepdims=True)
    y = (xf - mu) * lax.rsqrt(var + EPS)
    return (y * g.astype(jnp.float32) + b.astype(jnp.float32)).astype(x.dtype)


def _causal_dwconv(h, state, w):
    k = w.shape[0]
    ext = jnp.concatenate([state.astype(h.dtype), h], axis=1)
    y = lax.conv_general_dilated(
        ext, w[:, None, :].astype(h.dtype), window_strides=(1,), padding="VALID",
        dimension_numbers=("NWC", "WIO", "NWC"), feature_group_count=h.shape[-1])
    return y, ext[:, ext.shape[1] - (k - 1):, :]


def _pool_mix(u, state, pos, w_grp, scale):
    n, t, _ = u.shape
    ext = jnp.concatenate([state.astype(u.dtype), u], axis=1)
    extf = ext.astype(jnp.float32)
    cs = jnp.concatenate([jnp.zeros_like(extf[:, :1]), jnp.cumsum(extf, axis=1)], axis=1)
    end = cs[:, POOL_STATE + 1:, :]
    outs = []
    for g, win in enumerate(POOL_WINDOWS):
        lo, hi = g * POOL_GROUP_DIM, (g + 1) * POOL_GROUP_DIM
        start = cs[:, POOL_STATE + 1 - win: POOL_STATE + 1 - win + t, lo:hi]
        cnt = jnp.minimum(pos + 1, win).astype(jnp.float32)[None, :, None]
        outs.append((end[..., lo:hi] - start) / cnt)
    pooled = (jnp.concatenate(outs, axis=-1) - extf[:, POOL_STATE:, :]).astype(u.dtype)
    pooled = pooled.reshape(n, t, POOL_GROUPS, POOL_GROUP_DIM)
    mixed = jnp.einsum("btgc,gcd->btgd", pooled, w_grp).reshape(n, t, POOL_WIDTH)
    return mixed * scale, ext[:, ext.shape[1] - POOL_STATE:, :]


def _layer(x, pos, st_pool, st_sconv, st_cconv,
           norm_mix, w_in, pool_w, pool_scale, sconv_w, cconv_w, cconv_b,
           cconv_ln_g, cconv_ln_b, w_out, norm_ffn, w_gate, w_up, w_down):
    h = _rmsnorm(x, norm_mix)
    z = jnp.einsum("btd,de->bte", h, w_in)
    o1 = POOL_WIDTH
    o2 = o1 + 3 * SCONV_WIDTH
    u_a = z[..., :o1]
    b_gate, c_gate, x_b = jnp.split(z[..., o1:o2], 3, axis=-1)
    a_c, g_c = jnp.split(z[..., o2:], 2, axis=-1)
    y_a, ns_pool = _pool_mix(u_a, st_pool, pos, pool_w, pool_scale)
    conv_b, ns_sconv = _causal_dwconv(c_gate * x_b, st_sconv, sconv_w)
    y_b = b_gate * conv_b
    glu = a_c * jax.nn.sigmoid(g_c)
    conv_c, ns_cconv = _causal_dwconv(glu, st_cconv, cconv_w)
    y_c = jax.nn.silu(_layernorm(conv_c + cconv_b, cconv_ln_g, cconv_ln_b))
    mix = jnp.concatenate([y_a, y_b, y_c], axis=-1)
    x = x + jnp.einsum("btd,de->bte", mix, w_out)
    h2 = _rmsnorm(x, norm_ffn)
    f = jax.nn.silu(jnp.einsum("btd,df->btf", h2, w_gate)) * jnp.einsum("btd,df->btf", h2, w_up)
    x = x + jnp.einsum("btf,fd->btd", f, w_down)
    return x, ns_pool, ns_sconv, ns_cconv


def _trunk(x, pos, st_pool, st_sconv, st_cconv, layer_params, norm_final):
    sp, ss, sc = [], [], []
    for l in range(DEPTH):
        x, p, s, c = _layer(x, pos, st_pool[l], st_sconv[l], st_cconv[l],
                            *[w[l] for w in layer_params])
        sp.append(p); ss.append(s); sc.append(c)
    return _rmsnorm(x, norm_final), jnp.stack(sp), jnp.stack(ss), jnp.stack(sc)


def setup_inputs(seed: int = 0) -> dict:
    key = jax.random.key(seed)
    ks = jax.random.split(key, 24)
    f32 = jnp.float32
    nrm = lambda k, s, sc: jax.random.normal(k, s, f32) * sc
    return {
        "x_prompt": nrm(ks[0], (BATCH, SEQ, D_MODEL), 1.0),
        "x_sample": nrm(ks[1], (DEC_BATCH, DEC_SEQ, D_MODEL), 1.0),
        "state_pool": nrm(ks[2], (DEPTH, DEC_BATCH, POOL_STATE, POOL_WIDTH), 1.0),
        "state_sconv": nrm(ks[3], (DEPTH, DEC_BATCH, SCONV_K - 1, SCONV_WIDTH), 1.0),
        "state_cconv": nrm(ks[4], (DEPTH, DEC_BATCH, CCONV_K - 1, CCONV_WIDTH), 1.0),
        "norm_mix": 1.0 + nrm(ks[5], (DEPTH, D_MODEL), 0.02),
        "w_in": nrm(ks[6], (DEPTH, D_MODEL, D_IN), D_MODEL ** -0.5),
        "pool_w": nrm(ks[7], (DEPTH, POOL_GROUPS, POOL_GROUP_DIM, POOL_GROUP_DIM), POOL_GROUP_DIM ** -0.5),
        "pool_scale": 1.0 + nrm(ks[8], (DEPTH, POOL_WIDTH), 0.02),
        "sconv_w": nrm(ks[9], (DEPTH, SCONV_K, SCONV_WIDTH), SCONV_K ** -0.5),
        "cconv_w": nrm(ks[10], (DEPTH, CCONV_K, CCONV_WIDTH), CCONV_K ** -0.5),
        "cconv_b": nrm(ks[11], (DEPTH, CCONV_WIDTH), 0.01),
        "cconv_ln_g": 1.0 + nrm(ks[12], (DEPTH, CCONV_WIDTH), 0.02),
        "cconv_ln_b": nrm(ks[13], (DEPTH, CCONV_WIDTH), 0.01),
        "w_out": nrm(ks[14], (DEPTH, D_MODEL, D_MODEL), D_MODEL ** -0.5),
        "norm_ffn": 1.0 + nrm(ks[15], (DEPTH, D_MODEL), 0.02),
        "w_gate": nrm(ks[16], (DEPTH, D_MODEL, D_FF), D_MODEL ** -0.5),
        "w_up": nrm(ks[17], (DEPTH, D_MODEL, D_FF), D_MODEL ** -0.5),
        "w_down": nrm(ks[18], (DEPTH, D_FF, D_MODEL), D_FF ** -0.5),
        "norm_final": 1.0 + nrm(ks[19], (D_MODEL,), 0.02),
    }


def reference(x_prompt, x_sample, state_pool, state_sconv, state_cconv,
              norm_mix, w_in, pool_w, pool_scale, sconv_w, cconv_w, cconv_b,
              cconv_ln_g, cconv_ln_b, w_out, norm_ffn, w_gate, w_up, w_down, norm_final):
    layer_params = (norm_mix, w_in, pool_w, pool_scale, sconv_w, cconv_w, cconv_b,
                    cconv_ln_g, cconv_ln_b, w_out, norm_ffn, w_gate, w_up, w_down)
    dt = x_prompt.dtype
    nb = x_prompt.shape[0]
    z_pool = jnp.zeros((DEPTH, nb, POOL_STATE, POOL_WIDTH), dt)
    z_sconv = jnp.zeros((DEPTH, nb, SCONV_K - 1, SCONV_WIDTH), dt)
    z_cconv = jnp.zeros((DEPTH, nb, CCONV_K - 1, CCONV_WIDTH), dt)
    pos_prompt = jnp.arange(x_prompt.shape[1], dtype=jnp.int32)
    y_prompt, sp_p, ss_p, sc_p = _trunk(x_prompt, pos_prompt, z_pool, z_sconv, z_cconv,
                                        layer_params, norm_final)
    pos_sample = PAST_LEN + jnp.arange(x_sample.shape[1], dtype=jnp.int32)
    y_sample, sp_s, ss_s, sc_s = _trunk(x_sample, pos_sample, state_pool, state_sconv, state_cconv,
                                        layer_params, norm_final)
    return (y_prompt, y_sample, sp_p, sp_s, ss_p, ss_s, sc_p, sc_s)
```

```python
import numpy as np
import concourse.bass as bass
import concourse.mybir as mybir
from concourse.bass_utils import run_bass_kernel_spmd

F32 = mybir.dt.float32
BF16 = mybir.dt.bfloat16
AF = mybir.ActivationFunctionType
ALU = mybir.AluOpType
AX = mybir.AxisListType

D = 2048
KC = 16
DIN = 4352
DFF = 5632
FC = 44
FG = 11
NGRP = 4
DEPTH = 4
NP_ = 1084
NS = 16
N = NP_ + NS
HALO = 120
TILES = [(0, 512), (512, 1024), (1024, 1100)]
NPT = [512, 512, 60]
EPS = 1e-6
WINS = (2, 4, 8, 16)

O_NM, O_PS, O_CB, O_LG, O_LB, O_NF, O_SW, O_CW = 0, 16, 20, 26, 32, 38, 54, 72
LV = 258
O_FIN = DEPTH * LV
O_IC = O_FIN + 16
O_EPS = O_IC + 64
NV = O_EPS + 1
NSLOT = 4
KD = 10


class Prog:
    def __init__(self):
        self.ops = []
        self.last_w = {}
        self.readers = {}
        self.streams = {}

    def op(self, eng, emit, reads=(), writes=(), stream=None):
        oid = len(self.ops)
        deps = {}
        for k in reads:
            w = self.last_w.get(k)
            if w is not None:
                deps[w] = True
        for k in writes:
            w = self.last_w.get(k)
            if w is not None and w not in deps:
                deps[w] = False
            for r in self.readers.get(k, ()):
                if r not in deps:
                    deps[r] = False
        for k in writes:
            self.last_w[k] = oid
            self.readers[k] = []
        for k in reads:
            self.readers.setdefault(k, []).append(oid)
        sidx = None
        if stream is not None:
            sidx = self.streams.get(stream, 0) + 1
            self.streams[stream] = sidx
        self.ops.append(dict(eng=eng, emit=emit, deps=deps, stream=stream, sidx=sidx, signal=False, waits=[]))
        return oid

    def finalize(self):
        engs = ["pe", "act", "dve", "pool", "sp"]
        known = {e: {} for e in engs}
        for oid, o in enumerate(self.ops):
            e = o["eng"]
            kn = known[e]
            need = {}
            for d, raw in o["deps"].items():
                do = self.ops[d]
                if do["stream"] is not None:
                    key = ("s", do["stream"])
                    val = do["sidx"]
                else:
                    if do["eng"] == e and o["stream"] is None:
                        if e == "pe":
                            continue
                    key = ("e", do["eng"])
                    val = d
                if kn.get(key, -1) >= val:
                    continue
                if need.get(key, -1) < val:
                    need[key] = val
            for key, val in need.items():
                kn[key] = val
                if key[0] == "e":
                    self.ops[val]["signal"] = True
                o["waits"].append((key, val))
        cnt = {e: 0 for e in engs}
        for o in self.ops:
            if o["stream"] is None and o["signal"]:
                cnt[o["eng"]] += 1
                o["sigval"] = cnt[o["eng"]]

    def emit(self, nc, sems, eng_name, eng):
        for o in self.ops:
            if o["eng"] != eng_name:
                continue
            for key, val in o["waits"]:
                if key[0] == "e":
                    eng.wait_ge(sems["e_" + key[1]], self.ops[val]["sigval"])
                else:
                    eng.wait_ge(sems["s_" + key[1]], 16 * val)
            if o["emit"] is None:
                continue
            ins = o["emit"](eng)
            if o["stream"] is not None:
                ins.then_inc(sems["s_" + o["stream"]], 16)
            elif o["signal"]:
                ins.then_inc(sems["e_" + eng_name], 1)


def build_nc(depth=DEPTH):
    nc = bass.Bass("TRN2", target_bir_lowering=False)
    P = Prog()

    xT = nc.dram_tensor("xT", [KC, 128, N], F32, kind="ExternalInput").ap()
    vecs_d = nc.dram_tensor("vecs", [128, NV], F32, kind="ExternalInput").ap()
    ident_d = nc.dram_tensor("ident", [128, 128], F32, kind="ExternalInput").ap()
    poolw_d = nc.dram_tensor("poolw", [depth * 4, 128, 128], F32, kind="ExternalInput").ap()
    w_in_d = nc.dram_tensor("w_in", [depth, D, DIN], F32, kind="ExternalInput").ap()
    w_out_d = nc.dram_tensor("w_out", [depth, D, D], F32, kind="ExternalInput").ap()
    w_gate_d = nc.dram_tensor("w_gate", [depth, D, DFF], F32, kind="ExternalInput").ap()
    w_up_d = nc.dram_tensor("w_up", [depth, D, DFF], F32, kind="ExternalInput").ap()
    w_down_d = nc.dram_tensor("w_down", [depth, DFF, D], F32, kind="ExternalInput").ap()
    stp_d = nc.dram_tensor("st_pool", [depth, 4, 128, NS * 15], F32, kind="ExternalInput").ap()
    sts_d = nc.dram_tensor("st_sconv", [depth, 6, 128, NS * 2], F32, kind="ExternalInput").ap()
    stc_d = nc.dram_tensor("st_cconv", [depth, 6, 128, NS * 30], F32, kind="ExternalInput").ap()

    yT = nc.dram_tensor("yT", [KC, 128, N], F32, kind="ExternalOutput").ap()
    o_pp = nc.dram_tensor("o_pp", [128, depth * 4 * 15], F32, kind="ExternalOutput").ap()
    o_sp = nc.dram_tensor("o_sp", [128, depth * 6 * 2], F32, kind="ExternalOutput").ap()
    o_cp = nc.dram_tensor("o_cp", [128, depth * 6 * 30], F32, kind="ExternalOutput").ap()
    o_ps = nc.dram_tensor("o_ps", [depth, 4, 128, NS * 15], F32, kind="ExternalOutput").ap()
    o_ss = nc.dram_tensor("o_ss", [depth, 6, 128, NS * 2], F32, kind="ExternalOutput").ap()
    o_cs = nc.dram_tensor("o_cs", [depth, 6, 128, NS * 30], F32, kind="ExternalOutput").ap()

    from contextlib import ExitStack
    es = ExitStack()

    def sb(name, shape, dt):
        return es.enter_context(nc.sbuf_tensor(name, shape, dt))

    x = sb("x", [128, KC, N], F32)
    h = sb("h", [128, KC, N], BF16)
    mix = sb("mix", [128, KC, N], BF16)
    slots = [sb("slot%d" % i, [128, KC, 128], BF16) for i in range(NSLOT)]
    vecs = sb("vecs_sb", [128, NV], F32)
    ident = sb("ident_sb", [128, 128], F32)
    ones = sb("ones_sb", [128, 128], BF16)
    poolw = sb("poolw_sb", [128, 4, 128], BF16)
    R4 = [sb("R4_%d" % i, [128, 1120], F32) for i in range(2)]
    G2 = [sb("G2_%d" % i, [128, 1120], BF16) for i in range(1)]
    T2 = [sb("T2_%d" % i, [128, 528], F32) for i in range(3)]
    B1 = [sb("B1_%d" % i, [128, 512], BF16) for i in range(3)]
    LT = [sb("LT_%d" % i, [128, 512], F32) for i in range(2)]
    diag = [sb("diag%d" % i, [128, 31, 128], BF16) for i in range(1)]
    stb = [sb("stb_%d" % i, [128, NS, 30], F32) for i in range(2)]
    prod = sb("prod", [128, NS, 30], F32)
    nsb = [sb("nsb_%d" % i, [128, NS, 30], F32) for i in range(2)]
    sm = [sb("sm_%d" % i, [128, NS], F32) for i in range(4)]
    stg_pp = sb("stg_pp", [128, depth * 4 * 15], F32)
    stg_sp = sb("stg_sp", [128, depth * 6 * 2], F32)
    stg_cp = sb("stg_cp", [128, depth * 6 * 30], F32)
    ps = [es.enter_context(nc.psum_tensor("ps%d" % i, [128, 512], F32)) for i in range(8)]

    cnt = {"main": 0, "aux": 0, "slot": 0, "t2": 0, "b1": 0, "stb": 0, "nsb": 0, "g2": 0, "diag": 0, "lt": 0}

    def rot(name, n):
        v = cnt[name]
        cnt[name] = v + 1
        return v % n

    def main_bank():
        return rot("main", 4)

    def aux_bank():
        return 4 + rot("aux", 2)

    def vcol(l, off, i):
        c = l * LV + off + i
        return vecs[:, c:c + 1]

    def act(out, in_, func, reads, writes, bias=None, scale=None):
        kw = {}
        if bias is not None:
            kw["bias"] = bias
        if scale is not None:
            kw["scale"] = scale
        P.op("act", lambda e: e.activation(out, in_, func, **kw), reads, writes)

    def tt(out, in0, in1, op, reads, writes):
        P.op("dve", lambda e: e.tensor_tensor(out, in0, in1, op), reads, writes)

    def ts(out, in0, s1, s2, op0, op1, reads, writes):
        if op1 is None:
            P.op("dve", lambda e: e.tensor_scalar(out, in0, s1, None, op0), reads, writes)
        else:
            P.op("dve", lambda e: e.tensor_scalar(out, in0, s1, s2, op0, op1), reads, writes)

    def stt(out, in0, s, in1, op0, op1, reads, writes):
        P.op("dve", lambda e: e.scalar_tensor_tensor(out, in0, s, in1, op0, op1), reads, writes)

    def cp(out, in_, reads, writes):
        P.op("dve", lambda e: e.tensor_copy(out, in_), reads, writes)

    def mm(out, lhsT, rhs, start, stop, reads, writes):
        P.op("pe", lambda e: e.matmul(out, lhsT, rhs, start=start, stop=stop), reads, writes)

    def load_w(src_ap, nk):
        s = rot("slot", NSLOT)
        dst = slots[s][:, 0:nk, :]
        P.op("pool", lambda e: e.dma_start(out=dst, in_=src_ap), reads=(), writes=[("slot", s)],
             stream="slot%d" % s)
        return s

    def wblk(wd, l, c0):
        return wd[l, :, c0:c0 + 128].rearrange("(k p) n -> p k n", p=128)

    P.op("sp", lambda e: e.dma_start(out=vecs[:], in_=vecs_d), writes=[("vecs",)], stream="c0")
    P.op("sp", lambda e: e.dma_start(out=ident[:], in_=ident_d), writes=[("ident",)], stream="c1")
    CG = [(0, 6), (6, 11), (11, 16)]
    for t, (a, b) in enumerate(TILES):
        for q, (c0, c1) in enumerate(CG):
            P.op(["sp", "act", "pool"][q], lambda e, a=a, b=b, c0=c0, c1=c1: e.dma_start(
                out=x[:, c0:c1, a:b], in_=xT[c0:c1, :, a:b].rearrange("c p n -> p c n")),
                writes=[("x", c, t) for c in range(c0, c1)], stream="xin%d_%d" % (t, q))
    P.op("dve", lambda e: e.memset(ones[:], 1.0), writes=[("ones",)])
    for i in range(2):
        P.op("dve", lambda e, i=i: e.memset(R4[i][:], 0.0), writes=[("R4", i, t) for t in range(4)])
    for i in range(1):
        P.op("dve", lambda e, i=i: e.memset(G2[i][:], 0.0), writes=[("G2", i, t) for t in range(4)])

    pending = []

    def defer(fn):
        pending.append(fn)

    def flush_one():
        if pending:
            pending.pop(0)()

    def flush_all():
        while pending:
            pending.pop(0)()

    SB = [5, 6, 7]

    def norm_stats_chunk(c):
        bis = []
        for t, (a, b) in enumerate(TILES):
            n = b - a
            bi = rot("b1", 3)
            bis.append(bi)
            act(B1[bi][:, :n], x[:, c, a:b], AF.Square, [("x", c, t)], [("B1", bi)], scale=float(D ** -0.5))

        def pe_part():
            for t, (a, b) in enumerate(TILES):
                n = b - a
                mm(ps[SB[t]][:, :n], ones[:], B1[bis[t]][:, :n], c == 0, c == KC - 1,
                   [("ones",), ("B1", bis[t])], [("ps", SB[t])])
        defer(pe_part)

    def norm_finish(g_fn, dst_fn, dst_key):
        rstd = R4[1]
        for t, (a, b) in enumerate(TILES):
            n = b - a
            act(rstd[:, a:b], ps[SB[t]][:, :n], AF.Sqrt, [("ps", SB[t]), ("vecs",)], [("R4", 1, t)],
                bias=vecs[:, O_EPS:O_EPS + 1])
        for t, (a, b) in enumerate(TILES):
            P.op("dve", lambda e, o_=rstd[:, a:b]: e.reciprocal(o_, o_), [("R4", 1, t)], [("R4", 1, t)])
            for c in range(KC):
                stt(dst_fn(c)[:, a:b], x[:, c, a:b], g_fn(c), rstd[:, a:b], ALU.mult, ALU.mult,
                    [("x", c, t), ("R4", 1, t), ("vecs",)], [(dst_key, c, t)])

    def proj(slot, src, src_key, nk, t, bank):
        a, b = TILES[t]
        n = b - a
        for k in range(nk):
            mm(ps[bank][:, :n], slots[slot][:, k, :], src[:, k, a:b], k == 0, k == nk - 1,
               [("slot", slot), (src_key, k, t)], [("ps", bank)])

    for t, (a, b) in enumerate(TILES):
        n = b - a
        for c in range(KC):
            bi = rot("b1", 3)
            act(B1[bi][:, :n], x[:, c, a:b], AF.Square, [("x", c, t)], [("B1", bi)], scale=float(D ** -0.5))
            mm(ps[SB[t]][:, :n], ones[:], B1[bi][:, :n], c == 0, c == KC - 1,
               [("ones",), ("B1", bi)], [("ps", SB[t])])

    for l in range(depth):
        P.op("pool", lambda e, src_=poolw_d[l * 4:(l + 1) * 4].rearrange("q c d -> c q d"): e.dma_start(
            out=poolw[:], in_=src_), writes=[("poolw",)], stream="c2")
        norm_finish(lambda c: vcol(l, O_NM, c), lambda c: h[:, c, :], "h")

        for c in range(6):
            s_a = load_w(wblk(w_in_d, l, 512 + 2304 + c * 128), KC)
            s_g = load_w(wblk(w_in_d, l, 512 + 2304 + 768 + c * 128), KC)
            si = rot("stb", 2)
            st = stb[si]
            P.op("sp", lambda e, st=st, src_=stc_d[l, c].rearrange("p (b r) -> p b r", r=30): e.dma_start(
                out=st[:, :, 0:30], in_=src_),
                writes=[("stb", si)], stream="stb%d" % si)
            gi = rot("g2", 1)
            glu = G2[gi]
            wv = vecs[:, l * LV + O_CW + c * 31:l * LV + O_CW + c * 31 + 31]
            di = rot("diag", 1)
            dg = diag[di]
            gs = sm[2]
            for t, (a, b) in enumerate(TILES):
                n = b - a
                npr = NPT[t]
                ba, bg = main_bank(), main_bank()
                proj(s_a, h, "h", KC, t, ba)
                proj(s_g, h, "h", KC, t, bg)
                flush_one()
                ti = rot("t2", 3)
                sig = T2[ti]
                act(sig[:, :n], ps[bg][:, :n], AF.Sigmoid, [("ps", bg)], [("T2", ti)])
                tt(glu[:, 30 + a:30 + a + npr], ps[ba][:, :npr], sig[:, :npr], ALU.mult,
                   [("ps", ba), ("T2", ti)], [("G2", gi, t)])
                if t == 0:
                    tt(dg[:], ident[:].unsqueeze(1).broadcast_to([128, 31, 128]),
                       wv.unsqueeze(2).broadcast_to([128, 31, 128]), ALU.mult, [("ident",), ("vecs",)],
                       [("diag", di)])
                if t == 2:
                    tt(gs[:], ps[ba][:, npr:n], sig[:, npr:n], ALU.mult, [("ps", ba), ("T2", ti)], [("sm", 2)])
                    o0 = (l * 6 + c) * 30
                    tt(stg_cp[:, o0:o0 + 30], ps[ba][:, npr - 30:npr], sig[:, npr - 30:npr], ALU.mult,
                       [("ps", ba), ("T2", ti)], [("stg_cp",)])

                ai = rot("lt", 2)
                acc = LT[ai]
                kacc = ("LT", ai)
                rkd = [("G2", gi, t)] + ([("G2", gi, t - 1)] if t > 0 else []) + [("vecs",)]
                ts(acc[:, :npr], glu[:, a:a + npr], wv[:, 0:1], None, ALU.mult, None, rkd, [kacc])
                for k in range(1, KD):
                    stt(acc[:, :npr], glu[:, a + k:a + k + npr], wv[:, k:k + 1], acc[:, :npr], ALU.mult, ALU.add,
                        rkd + [kacc], [kacc])

                def conv_part(t=t, a=a, npr=npr, c=c, gi=gi, glu=glu, dg=dg, di=di, acc=acc, kacc=kacc):
                    xb = aux_bank()
                    rk = [("G2", gi, t)] + ([("G2", gi, t - 1)] if t > 0 else []) + [("diag", di)]
                    for k in range(KD, 31):
                        mm(ps[xb][:, :npr], dg[:, k, :], glu[:, a + k:a + k + npr], k == KD, k == 30,
                           rk, [("ps", xb)])
                    stt(mix[:, 10 + c, a:a + npr], ps[xb][:, :npr], vcol(l, O_CB, c), acc[:, :npr], ALU.add, ALU.add,
                        [("ps", xb), ("vecs",), kacc], [("mix", 10 + c, t)])
                defer(conv_part)
                if t == 2:
                    P.op("dve", lambda e, st=st, wv=wv: e.tensor_tensor(
                        prod[:], st[:], wv[:, 0:30].unsqueeze(1).broadcast_to([128, NS, 30]), ALU.mult),
                        [("stb", si), ("vecs",)], [("prod",)])
                    red = sm[0]
                    P.op("dve", lambda e, red=red: e.tensor_reduce(red[:], prod[:], AX.X, ALU.add),
                         [("prod",)], [("sm", 0)])
                    stt(red[:], gs[:], wv[:, 30:31], red[:], ALU.mult, ALU.add,
                        [("sm", 2), ("sm", 0), ("vecs",)], [("sm", 0)])
                    ts(sm[3][:], red[:], vcol(l, O_CB, c), None, ALU.add, None,
                       [("sm", 0), ("vecs",)], [("sm", 3)])
                    ni = rot("nsb", 2)
                    ns_ = nsb[ni]
                    cp(ns_[:, :, 0:29], st[:, :, 1:30], [("stb", si)], [("nsb", ni)])
                    cp(ns_[:, :, 29], gs[:], [("sm", 2), ("nsb", ni)], [("nsb", ni)])
                    P.op("sp", lambda e, ns_=ns_, dst_=o_cs[l, c].rearrange("p (b r) -> p b r", r=30): e.dma_start(
                        out=dst_, in_=ns_[:, :, 0:30]),
                        reads=[("nsb", ni)], writes=[("out", "cs", l, c)], stream="nsb%d" % ni)

                    def samp_part(c=c):
                        cp(mix[:, 10 + c, NP_:N], sm[3][:], [("sm", 3), ("mix", 10 + c, 2)], [("mix", 10 + c, 2)])
                    samp = samp_part
            last = pending.pop()
            defer(lambda last=last, samp=samp: (last(), samp()))

        ln_units = []

        def mk_ln(t):
            a, b = TILES[t]
            n = b - a
            mu, rs = LT[0], LT[1]
            km, kr = ("LT", 0), ("LT", 1)

            def stats():
                for c in range(6):
                    bi = rot("b1", 3)
                    act(B1[bi][:, :n], mix[:, 10 + c, a:b], AF.Square, [("mix", 10 + c, t)], [("B1", bi)])
                    mm(ps[6][:, :n], ones[:], mix[:, 10 + c, a:b], c == 0, c == 5,
                       [("ones",), ("mix", 10 + c, t)], [("ps", 6)])
                    mm(ps[7][:, :n], ones[:], B1[bi][:, :n], c == 0, c == 5,
                       [("ones",), ("B1", bi)], [("ps", 7)])
                ts(mu[:, :n], ps[6][:, :n], 1.0 / 768, None, ALU.mult, None, [("ps", 6)], [km])
                tt(rs[:, :n], mu[:, :n], mu[:, :n], ALU.mult, [km], [kr])
                stt(rs[:, :n], ps[7][:, :n], 1.0 / 768, rs[:, :n], ALU.mult, ALU.subtract,
                    [("ps", 7), kr], [kr])
                act(rs[:, :n], rs[:, :n], AF.Sqrt, [kr, ("vecs",)], [kr], bias=vecs[:, O_EPS:O_EPS + 1])
                P.op("dve", lambda e, o_=rs[:, :n]: e.reciprocal(o_, o_), [kr], [kr])

            def apply(c0, c1):
                for c in range(c0, c1):
                    ti = rot("t2", 3)
                    tt(T2[ti][:, :n], mix[:, 10 + c, a:b], mu[:, :n], ALU.subtract,
                       [("mix", 10 + c, t), km], [("T2", ti)])
                    tt(T2[ti][:, :n], T2[ti][:, :n], rs[:, :n], ALU.mult, [("T2", ti), kr], [("T2", ti)])
                    act(mix[:, 10 + c, a:b], T2[ti][:, :n], AF.Silu, [("T2", ti), ("vecs",)],
                        [("mix", 10 + c, t)], bias=vcol(l, O_LB, c), scale=vcol(l, O_LG, c))
            ln_units.append(stats)
            ln_units.append(lambda: apply(0, 3))
            ln_units.append(lambda: apply(3, 6))
        for t in range(3):
            mk_ln(t)

        cxe = R4[0]
        sstep = 0
        convb = R4[1]
        for c in range(6):
            s_c = load_w(wblk(w_in_d, l, 512 + 768 + c * 128), KC)
            s_x = load_w(wblk(w_in_d, l, 512 + 1536 + c * 128), KC)
            s_b = load_w(wblk(w_in_d, l, 512 + c * 128), KC)
            si = rot("stb", 2)
            st = stb[si]
            P.op("sp", lambda e, st=st, src_=sts_d[l, c].rearrange("p (b r) -> p b r", r=2): e.dma_start(
                out=st[:, :, 0:2], in_=src_),
                writes=[("stb", si)], stream="stb%d" % si)
            w0, w1, w2 = (vcol(l, O_SW, k * 6 + c) for k in range(3))
            for t, (a, b) in enumerate(TILES):
                n = b - a
                npr = NPT[t]
                bc, bx = main_bank(), main_bank()
                proj(s_c, h, "h", KC, t, bc)
                proj(s_x, h, "h", KC, t, bx)
                flush_one()
                if sstep >= 1 and ln_units:
                    ln_units.pop(0)()
                sstep += 1
                ti = rot("t2", 3)
                cg = T2[ti]
                act(cg[:, :n], ps[bc][:, :n], AF.Copy, [("ps", bc)], [("T2", ti)])
                tt(cxe[:, 2 + a:2 + b], ps[bx][:, :n], cg[:, :n], ALU.mult,
                   [("ps", bx), ("T2", ti)], [("R4", 0, t)])
                rk = [("R4", 0, t)] + ([("R4", 0, t - 1)] if t > 0 else []) + [("vecs",)]
                ts(convb[:, a:a + npr], cxe[:, a:a + npr], w0, None, ALU.mult, None, rk, [("R4", 1, t)])
                stt(convb[:, a:a + npr], cxe[:, a + 1:a + 1 + npr], w1, convb[:, a:a + npr], ALU.mult, ALU.add,
                    rk + [("R4", 1, t)], [("R4", 1, t)])
                stt(convb[:, a:a + npr], cxe[:, a + 2:a + 2 + npr], w2, convb[:, a:a + npr], ALU.mult, ALU.add,
                    rk + [("R4", 1, t)], [("R4", 1, t)])
                if t == 2:
                    new = cxe[:, 2 + NP_:2 + N]
                    cs = convb[:, NP_:N]
                    ts(cs, st[:, :, 0], w0, None, ALU.mult, None, [("stb", si), ("vecs",)], [("R4", 1, 2)])
                    stt(cs, st[:, :, 1], w1, cs, ALU.mult, ALU.add, [("stb", si), ("vecs",), ("R4", 1, 2)],
                        [("R4", 1, 2)])
                    stt(cs, new, w2, cs, ALU.mult, ALU.add, [("R4", 0, 2), ("vecs",), ("R4", 1, 2)],
                        [("R4", 1, 2)])
                    ni = rot("nsb", 2)
                    ns_ = nsb[ni]
                    cp(ns_[:, :, 0], st[:, :, 1], [("stb", si)], [("nsb", ni)])
                    cp(ns_[:, :, 1], new, [("R4", 0, 2), ("nsb", ni)], [("nsb", ni)])
                    P.op("sp", lambda e, ns_=ns_, dst_=o_ss[l, c].rearrange("p (b r) -> p b r", r=2): e.dma_start(
                        out=dst_, in_=ns_[:, :, 0:2]),
                        reads=[("nsb", ni)], writes=[("out", "ss", l, c)], stream="nsb%d" % ni)
                    o0 = (l * 6 + c) * 2
                    cp(stg_sp[:, o0:o0 + 2], cxe[:, NP_:NP_ + 2], [("R4", 0, 2)], [("stg_sp",)])
            for t, (a, b) in enumerate(TILES):
                n = b - a
                bb = main_bank()
                proj(s_b, h, "h", KC, t, bb)
                tt(mix[:, 4 + c, a:b], ps[bb][:, :n], convb[:, a:b], ALU.mult,
                   [("ps", bb), ("R4", 1, t)], [("mix", 4 + c, t)])

        while ln_units:
            ln_units.pop(0)()
        P.op("dve", lambda e: e.memset(R4[0][:, 0:16], 0.0), [], [("R4", 0, 0)])
        uext = R4[0]
        for g in range(4):
            win = WINS[g]
            M = g + 1
            s = load_w(wblk(w_in_d, l, g * 128), KC)
            si = rot("stb", 2)
            st = stb[si]
            P.op("sp", lambda e, st=st, src_=stp_d[l, g].rearrange("p (b r) -> p b r", r=15): e.dma_start(
                out=st[:, :, 0:15], in_=src_),
                writes=[("stb", si)], stream="stb%d" % si)
            for t, (a, b) in enumerate(TILES):
                n = b - a
                bank = main_bank()
                proj(s, h, "h", KC, t, bank)
                flush_one()
                act(uext[:, 15 + a:15 + b], ps[bank][:, :n], AF.Copy, [("ps", bank)], [("R4", 0, t)])
                npr = NPT[t]
                cur = None
                curk = None
                for m in range(1, M + 1):
                    Lm = (1 << M) - (1 << m)
                    lo = 15 - Lm
                    hi = 15 + npr
                    sh = 1 << (m - 1)
                    ti = rot("t2", 3)
                    if m == 1:
                        tt(T2[ti][:, lo:hi], uext[:, a + lo:a + hi], uext[:, a + lo - sh:a + hi - sh], ALU.add,
                           [("R4", 0, t)] + ([("R4", 0, t - 1)] if t > 0 else []), [("T2", ti)])
                    else:
                        tt(T2[ti][:, lo:hi], cur[:, lo:hi], cur[:, lo - sh:hi - sh], ALU.add,
                           [curk], [("T2", ti)])
                    cur, curk = T2[ti], ("T2", ti)
                bi = rot("b1", 3)
                pooled = B1[bi]
                stt(pooled[:, :npr], cur[:, 15:15 + npr], 1.0 / win, uext[:, 15 + a:15 + a + npr],
                    ALU.mult, ALU.subtract, [curk, ("R4", 0, t)], [("B1", bi)])
                if t == 0:
                    ti2 = rot("t2", 3)
                    ic = vecs[:, O_IC + g * 16:O_IC + g * 16 + 16]
                    tt(T2[ti2][:, 0:16], cur[:, 15:31], ic, ALU.mult, [curk, ("vecs",)], [("T2", ti2)])
                    tt(pooled[:, 0:16], T2[ti2][:, 0:16], uext[:, 15:31], ALU.subtract,
                       [("T2", ti2), ("R4", 0, 0), ("B1", bi)], [("B1", bi)])
                if t == 2:
                    new = uext[:, 15 + NP_:15 + N]
                    red, tmp = sm[0], sm[1]
                    P.op("dve", lambda e, st=st, win=win, red=red: e.tensor_reduce(
                        red[:], st[:, :, 15 - (win - 1):15], AX.X, ALU.add),
                        [("stb", si)], [("sm", 0)])
                    ts(tmp[:], new, 1.0 / win - 1.0, None, ALU.mult, None, [("R4", 0, 2)], [("sm", 1)])
                    stt(pooled[:, npr:n], red[:], 1.0 / win, tmp[:], ALU.mult, ALU.add,
                        [("sm", 0), ("sm", 1), ("B1", bi)], [("B1", bi)])
                    ni = rot("nsb", 2)
                    ns_ = nsb[ni]
                    cp(ns_[:, :, 0:14], st[:, :, 1:15], [("stb", si)], [("nsb", ni)])
                    cp(ns_[:, :, 14], new, [("R4", 0, 2), ("nsb", ni)], [("nsb", ni)])
                    P.op("sp", lambda e, ns_=ns_, dst_=o_ps[l, g].rearrange("p (b r) -> p b r", r=15): e.dma_start(
                        out=dst_, in_=ns_[:, :, 0:15]),
                        reads=[("nsb", ni)], writes=[("out", "ps", l, g)], stream="nsb%d" % ni)
                    o0 = (l * 4 + g) * 15
                    cp(stg_pp[:, o0:o0 + 15], uext[:, NP_:NP_ + 15], [("R4", 0, 2), ("R4", 0, 1)], [("stg_pp",)])

                def poolmm(g=g, t=t, a=a, b=b, n=n, bi=bi, pooled=pooled):
                    xb = aux_bank()
                    mm(ps[xb][:, :n], poolw[:, g, :], pooled[:, :n], True, True,
                       [("poolw",), ("B1", bi)], [("ps", xb)])
                    act(mix[:, g, a:b], ps[xb][:, :n], AF.Identity, [("ps", xb), ("vecs",)], [("mix", g, t)],
                        scale=vcol(l, O_PS, g))
                defer(poolmm)
        flush_all()

        for i in range(KC):
            s = load_w(wblk(w_out_d, l, i * 128), KC)
            for t, (a, b) in enumerate(TILES):
                n = b - a
                bank = main_bank()
                proj(s, mix, "mix", KC, t, bank)
                tt(x[:, i, a:b], ps[bank][:, :n], x[:, i, a:b], ALU.add, [("ps", bank), ("x", i, t)], [("x", i, t)])
            flush_one()
            norm_stats_chunk(i)
        flush_all()

        norm_finish(lambda c: vcol(l, O_NF, c), lambda c: h[:, c, :], "h")
        for grp in range(NGRP):
            for jj in range(FG):
                j = grp * FG + jj
                s_g = load_w(wblk(w_gate_d, l, j * 128), KC)
                s_u = load_w(wblk(w_up_d, l, j * 128), KC)
                for t, (a, b) in enumerate(TILES):
                    n = b - a
                    bg, bu = main_bank(), main_bank()
                    proj(s_g, h, "h", KC, t, bg)
                    proj(s_u, h, "h", KC, t, bu)
                    ti = rot("t2", 3)
                    act(T2[ti][:, :n], ps[bg][:, :n], AF.Silu, [("ps", bg)], [("T2", ti)])
                    tt(mix[:, jj, a:b], ps[bu][:, :n], T2[ti][:, :n], ALU.mult,
                       [("ps", bu), ("T2", ti)], [("mix", jj, t)])
            for i in range(KC):
                src = w_down_d[l, grp * FG * 128:(grp + 1) * FG * 128, i * 128:(i + 1) * 128].rearrange(
                    "(k p) n -> p k n", p=128)
                s = load_w(src, FG)
                for t, (a, b) in enumerate(TILES):
                    n = b - a
                    bank = main_bank()
                    proj(s, mix, "mix", FG, t, bank)
                    tt(x[:, i, a:b], ps[bank][:, :n], x[:, i, a:b], ALU.add,
                       [("ps", bank), ("x", i, t)], [("x", i, t)])
                if grp == NGRP - 1:
                    flush_one()
                    norm_stats_chunk(i)
        flush_all()

    norm_finish(lambda c: vecs[:, O_FIN + c:O_FIN + c + 1], lambda c: x[:, c, :], "x")
    for t, (a, b) in enumerate(TILES):
        for q, (c0, c1) in enumerate(CG):
            P.op(["sp", "act", "pool"][q], lambda e, a=a, b=b, c0=c0, c1=c1: e.dma_start(
                out=yT[c0:c1, :, a:b].rearrange("c p n -> p c n"), in_=x[:, c0:c1, a:b]),
                reads=[("x", c, t) for c in range(c0, c1)], writes=[("out", "y", t, q)],
                stream="yout%d_%d" % (t, q))
    P.op("sp", lambda e: e.dma_start(out=o_pp, in_=stg_pp[:]), reads=[("stg_pp",)], writes=[("out", "pp")], stream="o1")
    P.op("sp", lambda e: e.dma_start(out=o_sp, in_=stg_sp[:]), reads=[("stg_sp",)], writes=[("out", "sp")], stream="o2")
    P.op("sp", lambda e: e.dma_start(out=o_cp, in_=stg_cp[:]), reads=[("stg_cp",)], writes=[("out", "cp")], stream="o3")
    outs = [k for k in P.last_w if k[0] == "out"]
    P.op("sp", None, reads=outs, writes=[])

    P.finalize()
    sems = {}
    for e in ["pe", "act", "dve", "pool", "sp"]:
        sems["e_" + e] = es.enter_context(nc.semaphore("e_" + e))
    for sname in P.streams:
        sems["s_" + sname] = es.enter_context(nc.semaphore("s_" + sname))
    with nc.Block() as block:
        @block.tensor
        def _(e):
            P.emit(nc, sems, "pe", e)

        @block.scalar
        def _(e):
            P.emit(nc, sems, "act", e)

        @block.vector
        def _(e):
            P.emit(nc, sems, "dve", e)

        @block.gpsimd
        def _(e):
            P.emit(nc, sems, "pool", e)

        @block.sync
        def _(e):
            P.emit(nc, sems, "sp", e)
    es.close()
    return nc


def prep_inputs(inp, depth=DEPTH, cores=range(8)):
    f = lambda k: np.asarray(inp[k], dtype=np.float32)
    x_prompt, x_sample = f("x_prompt"), f("x_sample")
    vecs = np.zeros((128, NV), np.float32)

    def put(off, v):
        v = np.asarray(v, np.float32).reshape(-1, 128)
        vecs[:, off:off + v.shape[0]] = v.T

    for l in range(depth):
        o = l * LV
        put(o + O_NM, f("norm_mix")[l])
        put(o + O_PS, f("pool_scale")[l])
        put(o + O_CB, f("cconv_b")[l])
        put(o + O_LG, f("cconv_ln_g")[l])
        put(o + O_LB, f("cconv_ln_b")[l])
        put(o + O_NF, f("norm_ffn")[l])
        sw = f("sconv_w")[l]
        vecs[:, o + O_SW:o + O_SW + 18] = sw.reshape(3, 6, 128).transpose(2, 0, 1).reshape(128, 18)
        cw = f("cconv_w")[l]
        vecs[:, o + O_CW:o + O_CW + 186] = cw.reshape(31, 6, 128).transpose(2, 1, 0).reshape(128, 186)
    put(O_FIN, f("norm_final"))
    for g, win in enumerate(WINS):
        vecs[:, O_IC + g * 16:O_IC + g * 16 + 16] = 1.0 / np.minimum(np.arange(16) + 1, win).astype(np.float32)
    vecs[:, O_EPS] = EPS
    ident = np.eye(128, dtype=np.float32)
    poolw = np.ascontiguousarray(f("pool_w")[:depth].reshape(depth * 4, 128, 128))
    shared = dict(vecs=vecs, ident=ident, poolw=poolw,
                  w_in=f("w_in")[:depth], w_out=f("w_out")[:depth], w_gate=f("w_gate")[:depth],
                  w_up=f("w_up")[:depth], w_down=f("w_down")[:depth])
    maps = []
    for c in cores:
        b, hh = c // 2, c % 2
        t0 = 0 if hh == 0 else 2048 - NP_
        xc = np.concatenate([x_prompt[b, t0:t0 + NP_, :], x_sample[c * NS:(c + 1) * NS, 0, :]], axis=0)
        m = dict(shared)
        m["xT"] = np.ascontiguousarray(xc.T.reshape(KC, 128, N))
        sl = slice(c * NS, (c + 1) * NS)
        sp = f("state_pool")[:depth, sl]
        m["st_pool"] = np.ascontiguousarray(
            sp.reshape(depth, NS, 15, 4, 128).transpose(0, 3, 4, 1, 2).reshape(depth, 4, 128, NS * 15))
        ss = f("state_sconv")[:depth, sl]
        m["st_sconv"] = np.ascontiguousarray(
            ss.reshape(depth, NS, 2, 6, 128).transpose(0, 3, 4, 1, 2).reshape(depth, 6, 128, NS * 2))
        sc = f("state_cconv")[:depth, sl]
        m["st_cconv"] = np.ascontiguousarray(
            sc.reshape(depth, NS, 30, 6, 128).transpose(0, 3, 4, 1, 2).reshape(depth, 6, 128, NS * 30))
        maps.append(m)
    return maps


def assemble(results, depth=DEPTH, cores=range(8)):
    nb = 4
    y_prompt = np.zeros((nb, 2048, D), np.float32)
    y_sample = np.zeros((128, 1, D), np.float32)
    sp_p = np.zeros((depth, nb, 15, 512), np.float32)
    sp_s = np.zeros((depth, 128, 15, 512), np.float32)
    ss_p = np.zeros((depth, nb, 2, 768), np.float32)
    ss_s = np.zeros((depth, 128, 2, 768), np.float32)
    sc_p = np.zeros((depth, nb, 30, 768), np.float32)
    sc_s = np.zeros((depth, 128, 30, 768), np.float32)
    for c, r in zip(cores, results):
        b, hh = c // 2, c % 2
        y = r["yT"].reshape(D, N).T
        if hh == 0:
            y_prompt[b, 0:NP_] = y[0:NP_]
        else:
            y_prompt[b, NP_:2048] = y[2 * NP_ - 2048:NP_]
        y_sample[c * NS:(c + 1) * NS, 0] = y[NP_:N]
        sl = slice(c * NS, (c + 1) * NS)
        sp_s[:, sl] = r["o_ps"].reshape(depth, 4, 128, NS, 15).transpose(0, 3, 4, 1, 2).reshape(depth, NS, 15, 512)
        ss_s[:, sl] = r["o_ss"].reshape(depth, 6, 128, NS, 2).transpose(0, 3, 4, 1, 2).reshape(depth, NS, 2, 768)
        sc_s[:, sl] = r["o_cs"].reshape(depth, 6, 128, NS, 30).transpose(0, 3, 4, 1, 2).reshape(depth, NS, 30, 768)
        if hh == 1:
            sp_p[:, b] = r["o_pp"].reshape(128, depth, 4, 15).transpose(1, 3, 2, 0).reshape(depth, 15, 512)
            ss_p[:, b] = r["o_sp"].reshape(128, depth, 6, 2).transpose(1, 3, 2, 0).reshape(depth, 2, 768)
            sc_p[:, b] = r["o_cp"].reshape(128, depth, 6, 30).transpose(1, 3, 2, 0).reshape(depth, 30, 768)
    return (y_prompt, y_sample, sp_p, sp_s, ss_p, ss_s, sc_p, sc_s)


_NC_CACHE = {}


def kernel(**inputs):
    if "nc" not in _NC_CACHE:
        _NC_CACHE["nc"] = build_nc(DEPTH)
    nc = _NC_CACHE["nc"]
    maps = prep_inputs(inputs)
    res = run_bass_kernel_spmd(nc, maps, core_ids=list(range(8)))
    return assemble(res.results)
```

```python
import numpy as np
import concourse.bass as bass
import concourse.mybir as mybir
from concourse.bass_utils import run_bass_kernel_spmd

F32 = mybir.dt.float32
BF16 = mybir.dt.bfloat16
AF = mybir.ActivationFunctionType
ALU = mybir.AluOpType
AX = mybir.AxisListType

D = 2048
KC = 16
DIN = 4352
DFF = 5632
FC = 44
FG = 11
NGRP = 4
DEPTH = 4
NP_ = 1084
NS = 16
N = NP_ + NS
HALO = 120
TILES = [(0, 367), (367, 734), (734, 1100)]
NPT = [367, 367, 350]
EPS = 1e-6
WINS = (2, 4, 8, 16)

O_NM, O_PS, O_CB, O_LG, O_LB, O_NF, O_SW, O_CW = 0, 16, 20, 26, 32, 38, 54, 72
LV = 258
O_FIN = DEPTH * LV
O_IC = O_FIN + 16
O_EPS = O_IC + 64
NV = O_EPS + 1
NSLOT = 4


class Prog:
    def __init__(self):
        self.ops = []
        self.last_w = {}
        self.readers = {}
        self.streams = {}

    def op(self, eng, emit, reads=(), writes=(), stream=None):
        oid = len(self.ops)
        deps = {}
        for k in reads:
            w = self.last_w.get(k)
            if w is not None:
                deps[w] = True
        for k in writes:
            w = self.last_w.get(k)
            if w is not None and w not in deps:
                deps[w] = False
            for r in self.readers.get(k, ()):
                if r not in deps:
                    deps[r] = False
        for k in writes:
            self.last_w[k] = oid
            self.readers[k] = []
        for k in reads:
            self.readers.setdefault(k, []).append(oid)
        sidx = None
        if stream is not None:
            sidx = self.streams.get(stream, 0) + 1
            self.streams[stream] = sidx
        self.ops.append(dict(eng=eng, emit=emit, deps=deps, stream=stream, sidx=sidx, signal=False, waits=[]))
        return oid

    def finalize(self):
        engs = ["pe", "act", "dve", "pool", "sp"]
        known = {e: {} for e in engs}
        for oid, o in enumerate(self.ops):
            e = o["eng"]
            kn = known[e]
            need = {}
            for d, raw in o["deps"].items():
                do = self.ops[d]
                if do["stream"] is not None:
                    key = ("s", do["stream"])
                    val = do["sidx"]
                else:
                    if do["eng"] == e and o["stream"] is None:
                        if e == "pe":
                            continue
                    key = ("e", do["eng"])
                    val = d
                if kn.get(key, -1) >= val:
                    continue
                if need.get(key, -1) < val:
                    need[key] = val
            for key, val in need.items():
                kn[key] = val
                if key[0] == "e":
                    self.ops[val]["signal"] = True
                o["waits"].append((key, val))
        cnt = {e: 0 for e in engs}
        for o in self.ops:
            if o["stream"] is None and o["signal"]:
                cnt[o["eng"]] += 1
                o["sigval"] = cnt[o["eng"]]

    def emit(self, nc, sems, eng_name, eng):
        for o in self.ops:
            if o["eng"] != eng_name:
                continue
            for key, val in o["waits"]:
                if key[0] == "e":
                    eng.wait_ge(sems["e_" + key[1]], self.ops[val]["sigval"])
                else:
                    eng.wait_ge(sems["s_" + key[1]], 16 * val)
            if o["emit"] is None:
                continue
            ins = o["emit"](eng)
            if o["stream"] is not None:
                ins.then_inc(sems["s_" + o["stream"]], 16)
            elif o["signal"]:
                ins.then_inc(sems["e_" + eng_name], 1)


def build_nc(depth=DEPTH):
    nc = bass.Bass("TRN2", target_bir_lowering=False)
    P = Prog()

    xT = nc.dram_tensor("xT", [KC, 128, N], F32, kind="ExternalInput").ap()
    vecs_d = nc.dram_tensor("vecs", [128, NV], F32, kind="ExternalInput").ap()
    ident_d = nc.dram_tensor("ident", [128, 128], F32, kind="ExternalInput").ap()
    poolw_d = nc.dram_tensor("poolw", [depth * 4, 128, 128], F32, kind="ExternalInput").ap()
    w_in_d = nc.dram_tensor("w_in", [depth, D, DIN], F32, kind="ExternalInput").ap()
    w_out_d = nc.dram_tensor("w_out", [depth, D, D], F32, kind="ExternalInput").ap()
    w_gate_d = nc.dram_tensor("w_gate", [depth, D, DFF], F32, kind="ExternalInput").ap()
    w_up_d = nc.dram_tensor("w_up", [depth, D, DFF], F32, kind="ExternalInput").ap()
    w_down_d = nc.dram_tensor("w_down", [depth, DFF, D], F32, kind="ExternalInput").ap()
    stp_d = nc.dram_tensor("st_pool", [depth, 4, 128, NS * 15], F32, kind="ExternalInput").ap()
    sts_d = nc.dram_tensor("st_sconv", [depth, 6, 128, NS * 2], F32, kind="ExternalInput").ap()
    stc_d = nc.dram_tensor("st_cconv", [depth, 6, 128, NS * 30], F32, kind="ExternalInput").ap()

    yT = nc.dram_tensor("yT", [KC, 128, N], F32, kind="ExternalOutput").ap()
    o_pp = nc.dram_tensor("o_pp", [128, depth * 4 * 15], F32, kind="ExternalOutput").ap()
    o_sp = nc.dram_tensor("o_sp", [128, depth * 6 * 2], F32, kind="ExternalOutput").ap()
    o_cp = nc.dram_tensor("o_cp", [128, depth * 6 * 30], F32, kind="ExternalOutput").ap()
    o_ps = nc.dram_tensor("o_ps", [depth, 4, 128, NS * 15], F32, kind="ExternalOutput").ap()
    o_ss = nc.dram_tensor("o_ss", [depth, 6, 128, NS * 2], F32, kind="ExternalOutput").ap()
    o_cs = nc.dram_tensor("o_cs", [depth, 6, 128, NS * 30], F32, kind="ExternalOutput").ap()

    from contextlib import ExitStack
    es = ExitStack()

    def sb(name, shape, dt):
        return es.enter_context(nc.sbuf_tensor(name, shape, dt))

    x = sb("x", [128, KC, N], F32)
    h = sb("h", [128, KC, N], BF16)
    mix = sb("mix", [128, KC, N], BF16)
    slots = [sb("slot%d" % i, [128, KC, 128], BF16) for i in range(NSLOT)]
    vecs = sb("vecs_sb", [128, NV], F32)
    ident = sb("ident_sb", [128, 128], F32)
    ones = sb("ones_sb", [128, 128], BF16)
    poolw = sb("poolw_sb", [128, 4, 128], BF16)
    R4 = [sb("R4_%d" % i, [128, 1120], F32) for i in range(2)]
    G2 = [sb("G2_%d" % i, [128, 1120], BF16) for i in range(1)]
    T2 = [sb("T2_%d" % i, [128, 528], F32) for i in range(3)]
    B1 = [sb("B1_%d" % i, [128, 512], BF16) for i in range(3)]
    LT = [sb("LT_%d" % i, [128, 512], F32) for i in range(2)]
    diag = [sb("diag%d" % i, [128, 31, 128], BF16) for i in range(1)]
    stb = [sb("stb_%d" % i, [128, NS, 30], F32) for i in range(2)]
    prod = sb("prod", [128, NS, 30], F32)
    nsb = [sb("nsb_%d" % i, [128, NS, 30], F32) for i in range(2)]
    sm = [sb("sm_%d" % i, [128, NS], F32) for i in range(4)]
    stg_pp = sb("stg_pp", [128, depth * 4 * 15], F32)
    stg_sp = sb("stg_sp", [128, depth * 6 * 2], F32)
    stg_cp = sb("stg_cp", [128, depth * 6 * 30], F32)
    ps = [es.enter_context(nc.psum_tensor("ps%d" % i, [128, 512], F32)) for i in range(8)]

    cnt = {"main": 0, "aux": 0, "slot": 0, "t2": 0, "b1": 0, "stb": 0, "nsb": 0, "g2": 0, "diag": 0}

    def rot(name, n):
        v = cnt[name]
        cnt[name] = v + 1
        return v % n

    def main_bank():
        return rot("main", 4)

    def aux_bank():
        return 4 + rot("aux", 2)

    def vcol(l, off, i):
        c = l * LV + off + i
        return vecs[:, c:c + 1]

    def act(out, in_, func, reads, writes, bias=None, scale=None):
        kw = {}
        if bias is not None:
            kw["bias"] = bias
        if scale is not None:
            kw["scale"] = scale
        P.op("act", lambda e: e.activation(out, in_, func, **kw), reads, writes)

    def tt(out, in0, in1, op, reads, writes):
        P.op("dve", lambda e: e.tensor_tensor(out, in0, in1, op), reads, writes)

    def ts(out, in0, s1, s2, op0, op1, reads, writes):
        if op1 is None:
            P.op("dve", lambda e: e.tensor_scalar(out, in0, s1, None, op0), reads, writes)
        else:
            P.op("dve", lambda e: e.tensor_scalar(out, in0, s1, s2, op0, op1), reads, writes)

    def stt(out, in0, s, in1, op0, op1, reads, writes):
        P.op("dve", lambda e: e.scalar_tensor_tensor(out, in0, s, in1, op0, op1), reads, writes)

    def cp(out, in_, reads, writes):
        P.op("dve", lambda e: e.tensor_copy(out, in_), reads, writes)

    def mm(out, lhsT, rhs, start, stop, reads, writes):
        P.op("pe", lambda e: e.matmul(out, lhsT, rhs, start=start, stop=stop), reads, writes)

    def load_w(src_ap, nk):
        s = rot("slot", NSLOT)
        dst = slots[s][:, 0:nk, :]
        P.op("pool", lambda e: e.dma_start(out=dst, in_=src_ap), reads=(), writes=[("slot", s)],
             stream="slot%d" % s)
        return s

    def wblk(wd, l, c0):
        return wd[l, :, c0:c0 + 128].rearrange("(k p) n -> p k n", p=128)

    P.op("sp", lambda e: e.dma_start(out=vecs[:], in_=vecs_d), writes=[("vecs",)], stream="c0")
    P.op("sp", lambda e: e.dma_start(out=ident[:], in_=ident_d), writes=[("ident",)], stream="c1")
    CG = [(0, 6), (6, 11), (11, 16)]
    for t, (a, b) in enumerate(TILES):
        for q, (c0, c1) in enumerate(CG):
            P.op(["sp", "act", "pool"][q], lambda e, a=a, b=b, c0=c0, c1=c1: e.dma_start(
                out=x[:, c0:c1, a:b], in_=xT[c0:c1, :, a:b].rearrange("c p n -> p c n")),
                writes=[("x", c, t) for c in range(c0, c1)], stream="xin%d_%d" % (t, q))
    P.op("dve", lambda e: e.memset(ones[:], 1.0), writes=[("ones",)])
    for i in range(2):
        P.op("dve", lambda e, i=i: e.memset(R4[i][:], 0.0), writes=[("R4", i, t) for t in range(4)])
    for i in range(1):
        P.op("dve", lambda e, i=i: e.memset(G2[i][:], 0.0), writes=[("G2", i, t) for t in range(4)])

    pending = []

    def defer(fn):
        pending.append(fn)

    def flush_one():
        if pending:
            pending.pop(0)()

    def flush_all():
        while pending:
            pending.pop(0)()

    SB = [5, 6, 7]

    def norm_stats_chunk(c):
        bis = []
        for t, (a, b) in enumerate(TILES):
            n = b - a
            bi = rot("b1", 3)
            bis.append(bi)
            act(B1[bi][:, :n], x[:, c, a:b], AF.Square, [("x", c, t)], [("B1", bi)], scale=float(D ** -0.5))

        def pe_part():
            for t, (a, b) in enumerate(TILES):
                n = b - a
                mm(ps[SB[t]][:, :n], ones[:], B1[bis[t]][:, :n], c == 0, c == KC - 1,
                   [("ones",), ("B1", bis[t])], [("ps", SB[t])])
        defer(pe_part)

    def norm_finish(g_fn, dst_fn, dst_key):
        rstd = R4[1]
        for t, (a, b) in enumerate(TILES):
            n = b - a
            act(rstd[:, a:b], ps[SB[t]][:, :n], AF.Sqrt, [("ps", SB[t]), ("vecs",)], [("R4", 1, t)],
                bias=vecs[:, O_EPS:O_EPS + 1])
        for t, (a, b) in enumerate(TILES):
            P.op("dve", lambda e, o_=rstd[:, a:b]: e.reciprocal(o_, o_), [("R4", 1, t)], [("R4", 1, t)])
            for c in range(KC):
                stt(dst_fn(c)[:, a:b], x[:, c, a:b], g_fn(c), rstd[:, a:b], ALU.mult, ALU.mult,
                    [("x", c, t), ("R4", 1, t), ("vecs",)], [(dst_key, c, t)])

    def proj(slot, src, src_key, nk, t, bank):
        a, b = TILES[t]
        n = b - a
        for k in range(nk):
            mm(ps[bank][:, :n], slots[slot][:, k, :], src[:, k, a:b], k == 0, k == nk - 1,
               [("slot", slot), (src_key, k, t)], [("ps", bank)])

    for t, (a, b) in enumerate(TILES):
        n = b - a
        for c in range(KC):
            bi = rot("b1", 3)
            act(B1[bi][:, :n], x[:, c, a:b], AF.Square, [("x", c, t)], [("B1", bi)], scale=float(D ** -0.5))
            mm(ps[SB[t]][:, :n], ones[:], B1[bi][:, :n], c == 0, c == KC - 1,
               [("ones",), ("B1", bi)], [("ps", SB[t])])

    for l in range(depth):
        P.op("pool", lambda e, src_=poolw_d[l * 4:(l + 1) * 4].rearrange("q c d -> c q d"): e.dma_start(
            out=poolw[:], in_=src_), writes=[("poolw",)], stream="c2")
        norm_finish(lambda c: vcol(l, O_NM, c), lambda c: h[:, c, :], "h")

        for c in range(6):
            s_a = load_w(wblk(w_in_d, l, 512 + 2304 + c * 128), KC)
            s_g = load_w(wblk(w_in_d, l, 512 + 2304 + 768 + c * 128), KC)
            si = rot("stb", 2)
            st = stb[si]
            P.op("sp", lambda e, st=st, src_=stc_d[l, c].rearrange("p (b r) -> p b r", r=30): e.dma_start(
                out=st[:, :, 0:30], in_=src_),
                writes=[("stb", si)], stream="stb%d" % si)
            gi = rot("g2", 1)
            glu = G2[gi]
            wv = vecs[:, l * LV + O_CW + c * 31:l * LV + O_CW + c * 31 + 31]
            di = rot("diag", 1)
            dg = diag[di]
            gs = sm[2]
            for t, (a, b) in enumerate(TILES):
                n = b - a
                npr = NPT[t]
                ba, bg = main_bank(), main_bank()
                proj(s_a, h, "h", KC, t, ba)
                proj(s_g, h, "h", KC, t, bg)
                flush_one()
                ti = rot("t2", 3)
                sig = T2[ti]
                act(sig[:, :n], ps[bg][:, :n], AF.Sigmoid, [("ps", bg)], [("T2", ti)])
                tt(glu[:, 30 + a:30 + a + npr], ps[ba][:, :npr], sig[:, :npr], ALU.mult,
                   [("ps", ba), ("T2", ti)], [("G2", gi, t)])
                if t == 0:
                    tt(dg[:], ident[:].unsqueeze(1).broadcast_to([128, 31, 128]),
                       wv.unsqueeze(2).broadcast_to([128, 31, 128]), ALU.mult, [("ident",), ("vecs",)],
                       [("diag", di)])
                if t == 2:
                    tt(gs[:], ps[ba][:, npr:n], sig[:, npr:n], ALU.mult, [("ps", ba), ("T2", ti)], [("sm", 2)])
                    o0 = (l * 6 + c) * 30
                    tt(stg_cp[:, o0:o0 + 30], ps[ba][:, npr - 30:npr], sig[:, npr - 30:npr], ALU.mult,
                       [("ps", ba), ("T2", ti)], [("stg_cp",)])

                def conv_part(t=t, a=a, npr=npr, c=c, gi=gi, glu=glu, dg=dg, di=di):
                    xb = aux_bank()
                    rk = [("G2", gi, t)] + ([("G2", gi, t - 1)] if t > 0 else []) + [("diag", di)]
                    for k in range(31):
                        mm(ps[xb][:, :npr], dg[:, k, :], glu[:, a + k:a + k + npr], k == 0, k == 30,
                           rk, [("ps", xb)])
                    act(mix[:, 10 + c, a:a + npr], ps[xb][:, :npr], AF.Identity, [("ps", xb), ("vecs",)],
                        [("mix", 10 + c, t)], bias=vcol(l, O_CB, c))
                defer(conv_part)
                if t == 2:
                    P.op("dve", lambda e, st=st, wv=wv: e.tensor_tensor(
                        prod[:], st[:], wv[:, 0:30].unsqueeze(1).broadcast_to([128, NS, 30]), ALU.mult),
                        [("stb", si), ("vecs",)], [("prod",)])
                    red = sm[0]
                    P.op("dve", lambda e, red=red: e.tensor_reduce(red[:], prod[:], AX.X, ALU.add),
                         [("prod",)], [("sm", 0)])
                    stt(red[:], gs[:], wv[:, 30:31], red[:], ALU.mult, ALU.add,
                        [("sm", 2), ("sm", 0), ("vecs",)], [("sm", 0)])
                    ts(sm[3][:], red[:], vcol(l, O_CB, c), None, ALU.add, None,
                       [("sm", 0), ("vecs",)], [("sm", 3)])
                    ni = rot("nsb", 2)
                    ns_ = nsb[ni]
                    cp(ns_[:, :, 0:29], st[:, :, 1:30], [("stb", si)], [("nsb", ni)])
                    cp(ns_[:, :, 29], gs[:], [("sm", 2), ("nsb", ni)], [("nsb", ni)])
                    P.op("sp", lambda e, ns_=ns_, dst_=o_cs[l, c].rearrange("p (b r) -> p b r", r=30): e.dma_start(
                        out=dst_, in_=ns_[:, :, 0:30]),
                        reads=[("nsb", ni)], writes=[("out", "cs", l, c)], stream="nsb%d" % ni)

                    def samp_part(c=c):
                        cp(mix[:, 10 + c, NP_:N], sm[3][:], [("sm", 3), ("mix", 10 + c, 2)], [("mix", 10 + c, 2)])
                    samp = samp_part
            last = pending.pop()
            defer(lambda last=last, samp=samp: (last(), samp()))

        ln_units = []

        def mk_ln(t):
            a, b = TILES[t]
            n = b - a
            mu, rs = LT[0], LT[1]
            km, kr = ("LT", 0), ("LT", 1)

            def stats():
                for c in range(6):
                    bi = rot("b1", 3)
                    act(B1[bi][:, :n], mix[:, 10 + c, a:b], AF.Square, [("mix", 10 + c, t)], [("B1", bi)])
                    mm(ps[6][:, :n], ones[:], mix[:, 10 + c, a:b], c == 0, c == 5,
                       [("ones",), ("mix", 10 + c, t)], [("ps", 6)])
                    mm(ps[7][:, :n], ones[:], B1[bi][:, :n], c == 0, c == 5,
                       [("ones",), ("B1", bi)], [("ps", 7)])
                ts(mu[:, :n], ps[6][:, :n], 1.0 / 768, None, ALU.mult, None, [("ps", 6)], [km])
                tt(rs[:, :n], mu[:, :n], mu[:, :n], ALU.mult, [km], [kr])
                stt(rs[:, :n], ps[7][:, :n], 1.0 / 768, rs[:, :n], ALU.mult, ALU.subtract,
                    [("ps", 7), kr], [kr])
                act(rs[:, :n], rs[:, :n], AF.Sqrt, [kr, ("vecs",)], [kr], bias=vecs[:, O_EPS:O_EPS + 1])
                P.op("dve", lambda e, o_=rs[:, :n]: e.reciprocal(o_, o_), [kr], [kr])

            def apply(c0, c1):
                for c in range(c0, c1):
                    ti = rot("t2", 3)
                    tt(T2[ti][:, :n], mix[:, 10 + c, a:b], mu[:, :n], ALU.subtract,
                       [("mix", 10 + c, t), km], [("T2", ti)])
                    tt(T2[ti][:, :n], T2[ti][:, :n], rs[:, :n], ALU.mult, [("T2", ti), kr], [("T2", ti)])
                    act(mix[:, 10 + c, a:b], T2[ti][:, :n], AF.Silu, [("T2", ti), ("vecs",)],
                        [("mix", 10 + c, t)], bias=vcol(l, O_LB, c), scale=vcol(l, O_LG, c))
            ln_units.append(stats)
            ln_units.append(lambda: apply(0, 3))
            ln_units.append(lambda: apply(3, 6))
        for t in range(3):
            mk_ln(t)

        cxe = R4[0]
        sstep = 0
        convb = R4[1]
        for c in range(6):
            s_c = load_w(wblk(w_in_d, l, 512 + 768 + c * 128), KC)
            s_x = load_w(wblk(w_in_d, l, 512 + 1536 + c * 128), KC)
            s_b = load_w(wblk(w_in_d, l, 512 + c * 128), KC)
            si = rot("stb", 2)
            st = stb[si]
            P.op("sp", lambda e, st=st, src_=sts_d[l, c].rearrange("p (b r) -> p b r", r=2): e.dma_start(
                out=st[:, :, 0:2], in_=src_),
                writes=[("stb", si)], stream="stb%d" % si)
            w0, w1, w2 = (vcol(l, O_SW, k * 6 + c) for k in range(3))
            for t, (a, b) in enumerate(TILES):
                n = b - a
                npr = NPT[t]
                bc, bx = main_bank(), main_bank()
                proj(s_c, h, "h", KC, t, bc)
                proj(s_x, h, "h", KC, t, bx)
                flush_one()
                if sstep >= 1 and ln_units:
                    ln_units.pop(0)()
                sstep += 1
                ti = rot("t2", 3)
                cg = T2[ti]
                act(cg[:, :n], ps[bc][:, :n], AF.Copy, [("ps", bc)], [("T2", ti)])
                tt(cxe[:, 2 + a:2 + b], ps[bx][:, :n], cg[:, :n], ALU.mult,
                   [("ps", bx), ("T2", ti)], [("R4", 0, t)])
                rk = [("R4", 0, t)] + ([("R4", 0, t - 1)] if t > 0 else []) + [("vecs",)]
                ts(convb[:, a:a + npr], cxe[:, a:a + npr], w0, None, ALU.mult, None, rk, [("R4", 1, t)])
                stt(convb[:, a:a + npr], cxe[:, a + 1:a + 1 + npr], w1, convb[:, a:a + npr], ALU.mult, ALU.add,
                    rk + [("R4", 1, t)], [("R4", 1, t)])
                stt(convb[:, a:a + npr], cxe[:, a + 2:a + 2 + npr], w2, convb[:, a:a + npr], ALU.mult, ALU.add,
                    rk + [("R4", 1, t)], [("R4", 1, t)])
                if t == 2:
                    new = cxe[:, 2 + NP_:2 + N]
                    cs = convb[:, NP_:N]
                    ts(cs, st[:, :, 0], w0, None, ALU.mult, None, [("stb", si), ("vecs",)], [("R4", 1, 2)])
                    stt(cs, st[:, :, 1], w1, cs, ALU.mult, ALU.add, [("stb", si), ("vecs",), ("R4", 1, 2)],
                        [("R4", 1, 2)])
                    stt(cs, new, w2, cs, ALU.mult, ALU.add, [("R4", 0, 2), ("vecs",), ("R4", 1, 2)],
                        [("R4", 1, 2)])
                    ni = rot("nsb", 2)
                    ns_ = nsb[ni]
                    cp(ns_[:, :, 0], st[:, :, 1], [("stb", si)], [("nsb", ni)])
                    cp(ns_[:, :, 1], new, [("R4", 0, 2), ("nsb", ni)], [("nsb", ni)])
                    P.op("sp", lambda e, ns_=ns_, dst_=o_ss[l, c].rearrange("p (b r) -> p b r", r=2): e.dma_start(
                        out=dst_, in_=ns_[:, :, 0:2]),
                        reads=[("nsb", ni)], writes=[("out", "ss", l, c)], stream="nsb%d" % ni)
                    o0 = (l * 6 + c) * 2
                    cp(stg_sp[:, o0:o0 + 2], cxe[:, NP_:NP_ + 2], [("R4", 0, 2)], [("stg_sp",)])
            for t, (a, b) in enumerate(TILES):
                n = b - a
                bb = main_bank()
                proj(s_b, h, "h", KC, t, bb)
                tt(mix[:, 4 + c, a:b], ps[bb][:, :n], convb[:, a:b], ALU.mult,
                   [("ps", bb), ("R4", 1, t)], [("mix", 4 + c, t)])

        while ln_units:
            ln_units.pop(0)()
        P.op("dve", lambda e: e.memset(R4[0][:, 0:16], 0.0), [], [("R4", 0, 0)])
        uext = R4[0]
        for g in range(4):
            win = WINS[g]
            M = g + 1
            s = load_w(wblk(w_in_d, l, g * 128), KC)
            si = rot("stb", 2)
            st = stb[si]
            P.op("sp", lambda e, st=st, src_=stp_d[l, g].rearrange("p (b r) -> p b r", r=15): e.dma_start(
                out=st[:, :, 0:15], in_=src_),
                writes=[("stb", si)], stream="stb%d" % si)
            for t, (a, b) in enumerate(TILES):
                n = b - a
                bank = main_bank()
                proj(s, h, "h", KC, t, bank)
                flush_one()
                act(uext[:, 15 + a:15 + b], ps[bank][:, :n], AF.Copy, [("ps", bank)], [("R4", 0, t)])
                npr = NPT[t]
                cur = None
                curk = None
                for m in range(1, M + 1):
                    Lm = (1 << M) - (1 << m)
                    lo = 15 - Lm
                    hi = 15 + npr
                    sh = 1 << (m - 1)
                    ti = rot("t2", 3)
                    if m == 1:
                        tt(T2[ti][:, lo:hi], uext[:, a + lo:a + hi], uext[:, a + lo - sh:a + hi - sh], ALU.add,
                           [("R4", 0, t)] + ([("R4", 0, t - 1)] if t > 0 else []), [("T2", ti)])
                    else:
                        tt(T2[ti][:, lo:hi], cur[:, lo:hi], cur[:, lo - sh:hi - sh], ALU.add,
                           [curk], [("T2", ti)])
                    cur, curk = T2[ti], ("T2", ti)
                bi = rot("b1", 3)
                pooled = B1[bi]
                stt(pooled[:, :npr], cur[:, 15:15 + npr], 1.0 / win, uext[:, 15 + a:15 + a + npr],
                    ALU.mult, ALU.subtract, [curk, ("R4", 0, t)], [("B1", bi)])
                if t == 0:
                    ti2 = rot("t2", 3)
                    ic = vecs[:, O_IC + g * 16:O_IC + g * 16 + 16]
                    tt(T2[ti2][:, 0:16], cur[:, 15:31], ic, ALU.mult, [curk, ("vecs",)], [("T2", ti2)])
                    tt(pooled[:, 0:16], T2[ti2][:, 0:16], uext[:, 15:31], ALU.subtract,
                       [("T2", ti2), ("R4", 0, 0), ("B1", bi)], [("B1", bi)])
                if t == 2:
                    new = uext[:, 15 + NP_:15 + N]
                    red, tmp = sm[0], sm[1]
                    P.op("dve", lambda e, st=st, win=win, red=red: e.tensor_reduce(
                        red[:], st[:, :, 15 - (win - 1):15], AX.X, ALU.add),
                        [("stb", si)], [("sm", 0)])
                    ts(tmp[:], new, 1.0 / win - 1.0, None, ALU.mult, None, [("R4", 0, 2)], [("sm", 1)])
                    stt(pooled[:, npr:n], red[:], 1.0 / win, tmp[:], ALU.mult, ALU.add,
                        [("sm", 0), ("sm", 1), ("B1", bi)], [("B1", bi)])
                    ni = rot("nsb", 2)
                    ns_ = nsb[ni]
                    cp(ns_[:, :, 0:14], st[:, :, 1:15], [("stb", si)], [("nsb", ni)])
                    cp(ns_[:, :, 14], new, [("R4", 0, 2), ("nsb", ni)], [("nsb", ni)])
                    P.op("sp", lambda e, ns_=ns_, dst_=o_ps[l, g].rearrange("p (b r) -> p b r", r=15): e.dma_start(
                        out=dst_, in_=ns_[:, :, 0:15]),
                        reads=[("nsb", ni)], writes=[("out", "ps", l, g)], stream="nsb%d" % ni)
                    o0 = (l * 4 + g) * 15
                    cp(stg_pp[:, o0:o0 + 15], uext[:, NP_:NP_ + 15], [("R4", 0, 2), ("R4", 0, 1)], [("stg_pp",)])

                def poolmm(g=g, t=t, a=a, b=b, n=n, bi=bi, pooled=pooled):
                    xb = aux_bank()
                    mm(ps[xb][:, :n], poolw[:, g, :], pooled[:, :n], True, True,
                       [("poolw",), ("B1", bi)], [("ps", xb)])
                    act(mix[:, g, a:b], ps[xb][:, :n], AF.Identity, [("ps", xb), ("vecs",)], [("mix", g, t)],
                        scale=vcol(l, O_PS, g))
                defer(poolmm)
        flush_all()

        for i in range(KC):
            s = load_w(wblk(w_out_d, l, i * 128), KC)
            for t, (a, b) in enumerate(TILES):
                n = b - a
                bank = main_bank()
                proj(s, mix, "mix", KC, t, bank)
                tt(x[:, i, a:b], ps[bank][:, :n], x[:, i, a:b], ALU.add, [("ps", bank), ("x", i, t)], [("x", i, t)])
            flush_one()
            norm_stats_chunk(i)
        flush_all()

        norm_finish(lambda c: vcol(l, O_NF, c), lambda c: h[:, c, :], "h")
        for grp in range(NGRP):
            for jj in range(FG):
                j = grp * FG + jj
                s_g = load_w(wblk(w_gate_d, l, j * 128), KC)
                s_u = load_w(wblk(w_up_d, l, j * 128), KC)
                for t, (a, b) in enumerate(TILES):
                    n = b - a
                    bg, bu = main_bank(), main_bank()
                    proj(s_g, h, "h", KC, t, bg)
                    proj(s_u, h, "h", KC, t, bu)
                    ti = rot("t2", 3)
                    act(T2[ti][:, :n], ps[bg][:, :n], AF.Silu, [("ps", bg)], [("T2", ti)])
                    tt(mix[:, jj, a:b], ps[bu][:, :n], T2[ti][:, :n], ALU.mult,
                       [("ps", bu), ("T2", ti)], [("mix", jj, t)])
            for i in range(KC):
                src = w_down_d[l, grp * FG * 128:(grp + 1) * FG * 128, i * 128:(i + 1) * 128].rearrange(
                    "(k p) n -> p k n", p=128)
                s = load_w(src, FG)
                for t, (a, b) in enumerate(TILES):
                    n = b - a
                    bank = main_bank()
                    proj(s, mix, "mix", FG, t, bank)
                    tt(x[:, i, a:b], ps[bank][:, :n], x[:, i, a:b], ALU.add,
                       [("ps", bank), ("x", i, t)], [("x", i, t)])
                if grp == NGRP - 1:
                    flush_one()
                    norm_stats_chunk(i)
        flush_all()

    norm_finish(lambda c: vecs[:, O_FIN + c:O_FIN + c + 1], lambda c: x[:, c, :], "x")
    for t, (a, b) in enumerate(TILES):
        for q, (c0, c1) in enumerate(CG):
            P.op(["sp", "act", "pool"][q], lambda e, a=a, b=b, c0=c0, c1=c1: e.dma_start(
                out=yT[c0:c1, :, a:b].rearrange("c p n -> p c n"), in_=x[:, c0:c1, a:b]),
                reads=[("x", c, t) for c in range(c0, c1)], writes=[("out", "y", t, q)],
                stream="yout%d_%d" % (t, q))
    P.op("sp", lambda e: e.dma_start(out=o_pp, in_=stg_pp[:]), reads=[("stg_pp",)], writes=[("out", "pp")], stream="o1")
    P.op("sp", lambda e: e.dma_start(out=o_sp, in_=stg_sp[:]), reads=[("stg_sp",)], writes=[("out", "sp")], stream="o2")
    P.op("sp", lambda e: e.dma_start(out=o_cp, in_=stg_cp[:]), reads=[("stg_cp",)], writes=[("out", "cp")], stream="o3")
    outs = [k for k in P.last_w if k[0] == "out"]
    P.op("sp", None, reads=outs, writes=[])

    P.finalize()
    sems = {}
    for e in ["pe", "act", "dve", "pool", "sp"]:
        sems["e_" + e] = es.enter_context(nc.semaphore("e_" + e))
    for sname in P.streams:
        sems["s_" + sname] = es.enter_context(nc.semaphore("s_" + sname))
    with nc.Block() as block:
        @block.tensor
        def _(e):
            P.emit(nc, sems, "pe", e)

        @block.scalar
        def _(e):
            P.emit(nc, sems, "act", e)

        @block.vector
        def _(e):
            P.emit(nc, sems, "dve", e)

        @block.gpsimd
        def _(e):
            P.emit(nc, sems, "pool", e)

        @block.sync
        def _(e):
            P.emit(nc, sems, "sp", e)
    es.close()
    return nc


def prep_inputs(inp, depth=DEPTH, cores=range(8)):
    f = lambda k: np.asarray(inp[k], dtype=np.float32)
    x_prompt, x_sample = f("x_prompt"), f("x_sample")
    vecs = np.zeros((128, NV), np.float32)

    def put(off, v):
        v = np.asarray(v, np.float32).reshape(-1, 128)
        vecs[:, off:off + v.shape[0]] = v.T

    for l in range(depth):
        o = l * LV
        put(o + O_NM, f("norm_mix")[l])
        put(o + O_PS, f("pool_scale")[l])
        put(o + O_CB, f("cconv_b")[l])
        put(o + O_LG, f("cconv_ln_g")[l])
        put(o + O_LB, f("cconv_ln_b")[l])
        put(o + O_NF, f("norm_ffn")[l])
        sw = f("sconv_w")[l]
        vecs[:, o + O_SW:o + O_SW + 18] = sw.reshape(3, 6, 128).transpose(2, 0, 1).reshape(128, 18)
        cw = f("cconv_w")[l]
        vecs[:, o + O_CW:o + O_CW + 186] = cw.reshape(31, 6, 128).transpose(2, 1, 0).reshape(128, 186)
    put(O_FIN, f("norm_final"))
    for g, win in enumerate(WINS):
        vecs[:, O_IC + g * 16:O_IC + g * 16 + 16] = 1.0 / np.minimum(np.arange(16) + 1, win).astype(np.float32)
    vecs[:, O_EPS] = EPS
    ident = np.eye(128, dtype=np.float32)
    poolw = np.ascontiguousarray(f("pool_w")[:depth].reshape(depth * 4, 128, 128))
    shared = dict(vecs=vecs, ident=ident, poolw=poolw,
                  w_in=f("w_in")[:depth], w_out=f("w_out")[:depth], w_gate=f("w_gate")[:depth],
                  w_up=f("w_up")[:depth], w_down=f("w_down")[:depth])
    maps = []
    for c in cores:
        b, hh = c // 2, c % 2
        t0 = 0 if hh == 0 else 2048 - NP_
        xc = np.concatenate([x_prompt[b, t0:t0 + NP_, :], x_sample[c * NS:(c + 1) * NS, 0, :]], axis=0)
        m = dict(shared)
        m["xT"] = np.ascontiguousarray(xc.T.reshape(KC, 128, N))
        sl = slice(c * NS, (c + 1) * NS)
        sp = f("state_pool")[:depth, sl]
        m["st_pool"] = np.ascontiguousarray(
            sp.reshape(depth, NS, 15, 4, 128).transpose(0, 3, 4, 1, 2).reshape(depth, 4, 128, NS * 15))
        ss = f("state_sconv")[:depth, sl]
        m["st_sconv"] = np.ascontiguousarray(
            ss.reshape(depth, NS, 2, 6, 128).transpose(0, 3, 4, 1, 2).reshape(depth, 6, 128, NS * 2))
        sc = f("state_cconv")[:depth, sl]
        m["st_cconv"] = np.ascontiguousarray(
            sc.reshape(depth, NS, 30, 6, 128).transpose(0, 3, 4, 1, 2).reshape(depth, 6, 128, NS * 30))
        maps.append(m)
    return maps


def assemble(results, depth=DEPTH, cores=range(8)):
    nb = 4
    y_prompt = np.zeros((nb, 2048, D), np.float32)
    y_sample = np.zeros((128, 1, D), np.float32)
    sp_p = np.zeros((depth, nb, 15, 512), np.float32)
    sp_s = np.zeros((depth, 128, 15, 512), np.float32)
    ss_p = np.zeros((depth, nb, 2, 768), np.float32)
    ss_s = np.zeros((depth, 128, 2, 768), np.float32)
    sc_p = np.zeros((depth, nb, 30, 768), np.float32)
    sc_s = np.zeros((depth, 128, 30, 768), np.float32)
    for c, r in zip(cores, results):
        b, hh = c // 2, c % 2
        y = r["yT"].reshape(D, N).T
        if hh == 0:
            y_prompt[b, 0:NP_] = y[0:NP_]
        else:
            y_prompt[b, NP_:2048] = y[2 * NP_ - 2048:NP_]
        y_sample[c * NS:(c + 1) * NS, 0] = y[NP_:N]
        sl = slice(c * NS, (c + 1) * NS)
        sp_s[:, sl] = r["o_ps"].reshape(depth, 4, 128, NS, 15).transpose(0, 3, 4, 1, 2).reshape(depth, NS, 15, 512)
        ss_s[:, sl] = r["o_ss"].reshape(depth, 6, 128, NS, 2).transpose(0, 3, 4, 1, 2).reshape(depth, NS, 2, 768)
        sc_s[:, sl] = r["o_cs"].reshape(depth, 6, 128, NS, 30).transpose(0, 3, 4, 1, 2).reshape(depth, NS, 30, 768)
        if hh == 1:
            sp_p[:, b] = r["o_pp"].reshape(128, depth, 4, 15).transpose(1, 3, 2, 0).reshape(depth, 15, 512)
            ss_p[:, b] = r["o_sp"].reshape(128, depth, 6, 2).transpose(1, 3, 2, 0).reshape(depth, 2, 768)
            sc_p[:, b] = r["o_cp"].reshape(128, depth, 6, 30).transpose(1, 3, 2, 0).reshape(depth, 30, 768)
    return (y_prompt, y_sample, sp_p, sp_s, ss_p, ss_s, sc_p, sc_s)


_NC_CACHE = {}


def kernel(**inputs):
    if "nc" not in _NC_CACHE:
        _NC_CACHE["nc"] = build_nc(DEPTH)
    nc = _NC_CACHE["nc"]
    maps = prep_inputs(inputs)
    res = run_bass_kernel_spmd(nc, maps, core_ids=list(range(8)))
    return assemble(res.results)
```
